# Optimizing a Trainium2 kernel written in Bass

```python
import math
import jax, jax.numpy as jnp
from jax import lax
import numpy as np

D_MODEL = 1024
BATCH = 4
SEQ = 4096
DEPTH = 4

MEM_LEN = 256
EPS = 1e-6
BLOCK_Q = 128
CONV_W = 3
CONV_CH = D_MODEL // 2
DIFF_HEADS = 4
DIFF_HEAD_DIM = 64
DIFF_V_DIM = 2 * DIFF_HEAD_DIM
DIFF_QK = DIFF_HEADS * 2 * DIFF_HEAD_DIM
DIFF_WIDTH = DIFF_HEADS * DIFF_V_DIM
EVEN_IN = 3 * CONV_CH + 2 * DIFF_QK + DIFF_WIDTH
EVEN_OUT = CONV_CH + DIFF_WIDTH
SB_HEADS = 16
SB_HEAD_DIM = D_MODEL // SB_HEADS
X_HEADS = 4
X_HEAD_DIM = D_MODEL // X_HEADS
D_FF = 2816
N_EVEN = (DEPTH + 1) // 2
N_ODD = DEPTH // 2

kernel_name = "hybrid_conv_diffattn_stickbreak_trunk"


def rmsnorm(x, g):
    xf = x.astype(jnp.float32)
    y = xf * lax.rsqrt(jnp.mean(xf * xf, axis=-1, keepdims=True) + EPS)
    return (y * g.astype(jnp.float32)).astype(x.dtype)


def causal_dwconv(x, w):
    S = x.shape[1]
    xp = jnp.pad(x, ((0, 0), (CONV_W - 1, 0), (0, 0)))
    out = w[CONV_W - 1] * xp[:, CONV_W - 1:CONV_W - 1 + S]
    for k in range(CONV_W - 1):
        out = out + w[k] * xp[:, k:k + S]
    return out


def to_blocks(q):
    B, H, S, d = q.shape
    return q.reshape(B, H, S // BLOCK_Q, BLOCK_Q, d).transpose(2, 0, 1, 3, 4)


def from_blocks(o):
    NB, B, H, BQ, d = o.shape
    return o.transpose(1, 2, 0, 3, 4).reshape(B, H, NB * BQ, d)


def alibi_slopes(n_heads):
    return jnp.asarray(2.0 ** (-8.0 * np.arange(1, n_heads + 1) / n_heads), dtype=jnp.float32)


def diff_attention(q1, q2, k1, k2, v, lam):
    S = v.shape[2]
    nb = S // BLOCK_Q
    scale = DIFF_HEAD_DIM ** -0.5
    key_pos = jnp.arange(S)
    slopes = alibi_slopes(DIFF_HEADS)

    def block(args):
        i, qb1, qb2 = args
        q_pos = i * BLOCK_Q + jnp.arange(BLOCK_Q)
        dist = q_pos[:, None] - key_pos[None, :]
        causal = dist >= 0
        bias = -slopes[:, None, None] * jnp.abs(dist).astype(jnp.float32)

        def probs(qb, k):
            s = jnp.einsum('bhqd,bhkd->bhqk', qb, k).astype(jnp.float32) * scale + bias
            s = jnp.where(causal, s, -jnp.inf)
            return jax.nn.softmax(s, axis=-1)

        p = probs(qb1, k1) - lam * probs(qb2, k2)
        return jnp.einsum('bhqk,bhkd->bhqd', p.astype(v.dtype), v)

    o = lax.map(block, (jnp.arange(nb), to_blocks(q1), to_blocks(q2)))
    return from_blocks(o)


def stick_breaking_attention(q, k, v):
    S = v.shape[2]
    nb = S // BLOCK_Q
    scale = SB_HEAD_DIM ** -0.5
    key_pos = jnp.arange(S)

    def block(args):
        i, qb = args
        q_pos = i * BLOCK_Q + jnp.arange(BLOCK_Q)
        strict = key_pos[None, :] < q_pos[:, None]
        z = jnp.einsum('bhqd,bhkd->bhqk', qb, k).astype(jnp.float32) * scale
        log_beta = jax.nn.log_sigmoid(z)
        log_1m = jnp.where(strict, jax.nn.log_sigmoid(-z), 0.0)
        tail = lax.cumsum(log_1m, axis=3, reverse=True) - log_1m
        a = jnp.where(strict, jnp.exp(log_beta + tail), 0.0)
        return jnp.einsum('bhqk,bhkd->bhqd', a.astype(v.dtype), v)

    o = lax.map(block, (jnp.arange(nb), to_blocks(q)))
    return from_blocks(o)


def even_mixer(h, w_in, conv_a, q_gain, k_gain, lam_vecs, subln, w_out, lam_init):
    B, S, _ = h.shape
    proj = h @ w_in
    a_b, a_c, a_x, qd, kd, vd = jnp.split(
        proj, [CONV_CH, 2 * CONV_CH, 3 * CONV_CH, 3 * CONV_CH + DIFF_QK,
               3 * CONV_CH + 2 * DIFF_QK], axis=-1)
    y_a = a_b * causal_dwconv(a_c * a_x, conv_a)
    q = rmsnorm(qd.reshape(B, S, DIFF_HEADS, 2, DIFF_HEAD_DIM), q_gain)
    k = rmsnorm(kd.reshape(B, S, DIFF_HEADS, 2, DIFF_HEAD_DIM), k_gain)
    q = q.transpose(3, 0, 2, 1, 4)
    k = k.transpose(3, 0, 2, 1, 4)
    v = vd.reshape(B, S, DIFF_HEADS, DIFF_V_DIM).transpose(0, 2, 1, 3)
    lf = lam_vecs.astype(jnp.float32)
    lam = jnp.exp(jnp.sum(lf[0] * lf[1])) - jnp.exp(jnp.sum(lf[2] * lf[3])) + lam_init
    o = diff_attention(q[0], q[1], k[0], k[1], v, lam)
    o = rmsnorm(o, subln) * (1.0 - lam_init)
    y_b = o.transpose(0, 2, 1, 3).reshape(B, S, DIFF_WIDTH)
    return jnp.concatenate([y_a, y_b], axis=-1) @ w_out


def odd_mixer(h, w_qkv, w_out):
    B, S, _ = h.shape
    qkv = (h @ w_qkv).reshape(B, S, 3, SB_HEADS, SB_HEAD_DIM).transpose(2, 0, 3, 1, 4)
    o = stick_breaking_attention(qkv[0], qkv[1], qkv[2])
    return o.transpose(0, 2, 1, 3).reshape(B, S, D_MODEL) @ w_out


def mem_attention(h, mem_n, w_q, w_kv, q_gain, k_gain, w_o):
    B, S, _ = h.shape
    M = mem_n.shape[1]
    q = rmsnorm((h @ w_q).reshape(B, S, X_HEADS, X_HEAD_DIM), q_gain)
    kv = (mem_n @ w_kv).reshape(B, M, 2, X_HEADS, X_HEAD_DIM)
    k = rmsnorm(kv[:, :, 0], k_gain)
    v = kv[:, :, 1]
    s = jnp.einsum('bqhd,bkhd->bhqk', q, k).astype(jnp.float32) * (X_HEAD_DIM ** -0.5)
    p = jax.nn.softmax(s, axis=-1)
    o = jnp.einsum('bhqk,bkhd->bqhd', p.astype(v.dtype), v)
    return o.reshape(B, S, D_MODEL) @ w_o


def conv_ffn(h, w_up, w_conv, w_down):
    u = causal_dwconv(h @ w_up, w_conv)
    gate, val = jnp.split(u, 2, axis=-1)
    return (jax.nn.silu(gate) * val) @ w_down


def setup_inputs(seed: int = 0) -> dict:
    key = jax.random.key(seed)
    ks = jax.random.split(key, 24)
    f32 = jnp.float32

    def w(k, shape, fan_in):
        return jax.random.normal(k, shape, f32) * (fan_in ** -0.5)

    def gain(k, shape):
        return 1.0 + 0.02 * jax.random.normal(k, shape, f32)

    return {
        "x": jax.random.normal(ks[0], (BATCH, SEQ, D_MODEL), f32),
        "mem": jax.random.normal(ks[1], (BATCH, MEM_LEN, D_MODEL), f32),
        "norm_mix": gain(ks[2], (DEPTH, D_MODEL)),
        "norm_xattn": gain(ks[3], (DEPTH, D_MODEL)),
        "norm_mem": gain(ks[4], (DEPTH, D_MODEL)),
        "norm_ffn": gain(ks[5], (DEPTH, D_MODEL)),
        "even_w_in": w(ks[6], (N_EVEN, D_MODEL, EVEN_IN), D_MODEL),
        "even_conv": w(ks[7], (N_EVEN, CONV_W, CONV_CH), CONV_W),
        "even_q_gain": gain(ks[8], (N_EVEN, DIFF_HEAD_DIM)),
        "even_k_gain": gain(ks[9], (N_EVEN, DIFF_HEAD_DIM)),
        "even_lambda": 0.1 * jax.random.normal(ks[10], (N_EVEN, 4, DIFF_HEAD_DIM), f32),
        "even_subln": gain(ks[11], (N_EVEN, DIFF_V_DIM)),
        "even_w_out": w(ks[12], (N_EVEN, EVEN_OUT, D_MODEL), EVEN_OUT),
        "odd_w_qkv": w(ks[13], (N_ODD, D_MODEL, 3 * D_MODEL), D_MODEL),
        "odd_w_out": w(ks[14], (N_ODD, D_MODEL, D_MODEL), D_MODEL),
        "x_w_q": w(ks[15], (DEPTH, D_MODEL, D_MODEL), D_MODEL),
        "x_w_kv": w(ks[16], (DEPTH, D_MODEL, 2 * D_MODEL), D_MODEL),
        "x_q_gain": gain(ks[17], (DEPTH, X_HEAD_DIM)),
        "x_k_gain": gain(ks[18], (DEPTH, X_HEAD_DIM)),
        "x_w_out": w(ks[19], (DEPTH, D_MODEL, D_MODEL), D_MODEL),
        "ffn_w_up": w(ks[20], (DEPTH, D_MODEL, 2 * D_FF), D_MODEL),
        "ffn_conv": w(ks[21], (DEPTH, CONV_W, 2 * D_FF), CONV_W),
        "ffn_w_down": w(ks[22], (DEPTH, D_FF, D_MODEL), D_FF),
    }


def reference(x, mem, norm_mix, norm_xattn, norm_mem, norm_ffn,
              even_w_in, even_conv, even_q_gain, even_k_gain, even_lambda, even_subln, even_w_out,
              odd_w_qkv, odd_w_out,
              x_w_q, x_w_kv, x_q_gain, x_k_gain, x_w_out,
              ffn_w_up, ffn_conv, ffn_w_down):
    for l in range(DEPTH):
        h = rmsnorm(x, norm_mix[l])
        if l % 2 == 0:
            e = l // 2
            lam_init = 0.8 - 0.6 * math.exp(-0.3 * l)
            x = x + even_mixer(h, even_w_in[e], even_conv[e], even_q_gain[e], even_k_gain[e],
                               even_lambda[e], even_subln[e], even_w_out[e], lam_init)
        else:
            o = l // 2
            x = x + odd_mixer(h, odd_w_qkv[o], odd_w_out[o])
        x = x + mem_attention(rmsnorm(x, norm_xattn[l]), rmsnorm(mem, norm_mem[l]),
                              x_w_q[l], x_w_kv[l], x_q_gain[l], x_k_gain[l], x_w_out[l])
        x = x + conv_ffn(rmsnorm(x, norm_ffn[l]), ffn_w_up[l], ffn_conv[l], ffn_w_down[l])
    return x
```

```python
import contextlib
import math
import numpy as np
import concourse.bass as bass
import concourse.mybir as mybir
from concourse.bass_utils import run_bass_kernel_spmd

F32 = mybir.dt.float32
BF16 = mybir.dt.bfloat16
AF = mybir.ActivationFunctionType
ALU = mybir.AluOpType
AX = mybir.AxisListType

ENGS = ("sp", "act", "dve", "pool", "pe")
INF = 1 << 60
EPS = 1e-6

CFG = {"SEQ": 4096, "DEPTH": 4}
D = 1024
DFF = 2816
MEM = 256


class Buf:
    _n = 0

    def __init__(self, name, t):
        self.name = name
        self.t = t
        Buf._n += 1
        self.id = Buf._n
        self.reg = (self, 0, INF)

    @property
    def ap(self):
        return self.t

    def __getitem__(self, k):
        return self.t[k]

    def key(self, lo, hi=None):
        return View(None, (self, lo, lo + 1 if hi is None else hi))


class View:
    def __init__(self, ap, reg):
        self.ap = ap
        self.reg = reg

    def __getitem__(self, k):
        return self.ap[k]


class Ins:
    __slots__ = ("eng", "fn", "deps", "is_dma", "need_inc", "tok", "pos")

    def __init__(self, eng, fn, is_dma):
        self.eng = eng
        self.fn = fn
        self.deps = {}
        self.is_dma = is_dma
        self.need_inc = False
        self.tok = None
        self.pos = None


class Prog:
    N_DMA_SEMS = 48
    N_CC_SEMS = 8
    N_SW_SEMS = 12
    GEN = 30000

    def __init__(self, nc):
        self.nc = nc
        self.stack = contextlib.ExitStack()
        self.ins = {e: [] for e in ENGS}
        self.state = {}
        self.dma_prev = [None] * self.N_DMA_SEMS
        self.dma_rr = 0
        self.n_ins = 0
        self.cc_prev = [None] * self.N_CC_SEMS
        self.cc_rr = 0
        self.sw_rr = 0

    def sbuf(self, name, shape, dtype):
        return Buf(name, self.stack.enter_context(self.nc.sbuf_tensor(name, list(shape), dtype)))

    def psum(self, name, shape, dtype):
        return Buf(name, self.stack.enter_context(self.nc.psum_tensor(name, list(shape), dtype)))

    def dram(self, name, shape, dtype, kind="Internal"):
        return Buf(name, self.nc.dram_tensor(name, list(shape), dtype, kind=kind).ap())

    def _add_dep(self, ins, other):
        if other is None or other is ins:
            return
        if other.eng == "pe" and ins.eng == "pe" and not other.is_dma and not ins.is_dma:
            return
        key = (other.eng, other.is_dma and id(other))
        cur = ins.deps.get(key)
        if cur is None or cur.pos < other.pos:
            ins.deps[key] = other

    def _touch(self, ins, reads, writes):
        rk = (ins.eng, ins.is_dma and id(ins))
        for r in reads:
            b, lo, hi = r.reg
            st = self.state.setdefault(b.id, [])
            exact = None
            for ent in st:
                if ent[0] < hi and lo < ent[1]:
                    self._add_dep(ins, ent[2])
                    if ent[0] == lo and ent[1] == hi:
                        exact = ent
            if exact is None:
                exact = [lo, hi, None, {}]
                st.append(exact)
            exact[3][rk] = ins
        for w in writes:
            b, lo, hi = w.reg
            st = self.state.setdefault(b.id, [])
            keep = []
            for ent in st:
                if ent[0] < hi and lo < ent[1]:
                    self._add_dep(ins, ent[2])
                    for rd in ent[3].values():
                        self._add_dep(ins, rd)
                    if lo <= ent[0] and ent[1] <= hi:
                        continue
                keep.append(ent)
            keep.append([lo, hi, ins, {}])
            self.state[b.id] = keep

    def op(self, eng, fn, r=(), w=()):
        ins = Ins(eng, fn, False)
        ins.pos = self.n_ins
        self.n_ins += 1
        self._touch(ins, r, w)
        self.ins[eng].append(ins)
        return ins

    def dma(self, eng, out, in_, r=(), w=(), **kw):
        ins = Ins(eng, None, True)
        ins.pos = self.n_ins
        self.n_ins += 1
        ins.fn = lambda e: e.dma_start(out=out, in_=in_, **kw)
        self._touch(ins, r, w)
        if eng == "pool":
            slot = self.N_DMA_SEMS - self.N_SW_SEMS + self.sw_rr
            self.sw_rr = (self.sw_rr + 1) % self.N_SW_SEMS
        else:
            slot = self.dma_rr
            self.dma_rr = (self.dma_rr + 1) % (self.N_DMA_SEMS - self.N_SW_SEMS)
        prev = self.dma_prev[slot]
        if prev is not None:
            ins.deps[("dmaslot", id(prev))] = prev
        self.dma_prev[slot] = ins
        ins.tok = slot
        self.ins[eng].append(ins)
        return ins

    def collective(self, kind, rg, in_ap, out_ap, r=(), w=()):
        ins = Ins("pool", None, True)
        ins.pos = self.n_ins
        self.n_ins += 1
        ins.fn = lambda e: e.collective_compute(kind, ALU.bypass, replica_groups=rg, ins=[in_ap], outs=[out_ap])
        self._touch(ins, r, w)
        slot = self.cc_rr
        self.cc_rr = (self.cc_rr + 1) % self.N_CC_SEMS
        prev = self.cc_prev[slot]
        if prev is not None:
            ins.deps[("ccslot", id(prev))] = prev
        self.cc_prev[slot] = ins
        ins.tok = ("cc", slot)
        self.ins["pool"].append(ins)
        return ins

    def emit(self, final_wait=()):
        nc = self.nc
        allins = sorted((i for e in ENGS for i in self.ins[e]), key=lambda i: i.pos)
        for i in allins:
            for d in i.deps.values():
                d.need_inc = True
        for i in final_wait:
            i.need_inc = True
        dma_sems = [self.stack.enter_context(nc.semaphore(f"dq{j}")) for j in range(self.N_DMA_SEMS)]
        dma_cnt = [0] * self.N_DMA_SEMS
        cc_sems = [self.stack.enter_context(nc.semaphore(f"cc{j}")) for j in range(self.N_CC_SEMS)]
        cc_cnt = [0] * self.N_CC_SEMS
        eng_sem, eng_cnt, gen = {}, {}, {}
        for i in allins:
            if i.is_dma and isinstance(i.tok, tuple):
                slot = i.tok[1]
                cc_cnt[slot] += 1
                i.tok = (cc_sems[slot], cc_cnt[slot], "cc")
            elif i.is_dma:
                slot = i.tok
                dma_cnt[slot] += 16
                i.tok = (dma_sems[slot], dma_cnt[slot])
            elif i.need_inc:
                e = i.eng
                if e not in eng_sem or eng_cnt[e] >= self.GEN:
                    gen[e] = gen.get(e, -1) + 1
                    eng_sem[e] = self.stack.enter_context(nc.semaphore(f"s_{e}_{gen[e]}"))
                    eng_cnt[e] = 0
                eng_cnt[e] += 1
                i.tok = (eng_sem[e], eng_cnt[e])
        with nc.Block() as block:
            def body(e):
                def run(h):
                    waited = {}
                    for i in self.ins[e]:
                        for d in i.deps.values():
                            sem, val = d.tok[0], d.tok[1]
                            k = id(sem)
                            if waited.get(k, 0) >= val:
                                continue
                            waited[k] = val
                            h.wait_ge(sem, val)
                        x = i.fn(h)
                        if i.is_dma:
                            x.then_inc(i.tok[0], 1 if len(i.tok) == 3 else 16)
                        elif i.need_inc:
                            x.then_inc(i.tok[0], 1)
                    if e == "sp":
                        for i in final_wait:
                            h.wait_ge(i.tok[0], i.tok[1])
                return run
            block.sync(body("sp"))
            block.scalar(body("act"))
            block.vector(body("dve"))
            block.gpsimd(body("pool"))
            block.tensor(body("pe"))
        self.stack.close()


def MM(P, out, lhsT, rhs, start, stop, r, w):
    P.op("pe", lambda e: e.matmul(out, lhsT=lhsT, rhs=rhs, start=start, stop=stop), r=r, w=w)


def TR(P, out, in_, ident, r, w):
    P.op("pe", lambda e: e.transpose(out, in_, ident), r=r, w=w)


def ACT(P, out, in_, func, r, w, **kw):
    P.op("act", lambda e: e.activation(out=out, in_=in_, func=func, **kw), r=r, w=w)


def TS(P, eng, out, in0, s1, s2, op0, op1, r, w):
    if op1 is None:
        P.op(eng, lambda e: e.tensor_scalar(out, in0, s1, None, op0), r=r, w=w)
    else:
        P.op(eng, lambda e: e.tensor_scalar(out, in0, s1, s2, op0, op1), r=r, w=w)


def STT(P, out, in0, scalar, in1, op0, op1, r, w):
    P.op("dve", lambda e: e.scalar_tensor_tensor(out, in0, scalar, in1, op0, op1), r=r, w=w)


def TT(P, eng, out, in0, in1, op, r, w):
    P.op(eng, lambda e: e.tensor_tensor(out, in0, in1, op), r=r, w=w)


def CP(P, eng, out, in_, r, w):
    if eng == "act":
        P.op("act", lambda e: e.activation(out=out, in_=in_, func=AF.Copy), r=r, w=w)
    else:
        P.op(eng, lambda e: e.tensor_copy(out, in_), r=r, w=w)


class Ctx:
    pass


def alibi_slopes(n):
    return [2.0 ** (-8.0 * (i + 1) / n) for i in range(n)]


def build(S, DEPTH):
    NB = S // 128
    NBH = NB // 2
    NLB = NBH + 1
    TH = NBH * 128
    NT = NLB * 128
    EB = NB + 1
    NEV = (DEPTH + 1) // 2
    NOD = DEPTH // 2
    rg = [[0, 1], [2, 3], [4, 5], [6, 7]]

    nc = bass.Bass("TRN2", target_bir_lowering=False)
    P = Prog(nc)
    C = Ctx()

    def din(name, shape, dt=F32):
        return P.dram(name, shape, dt, kind="ExternalInput")

    x_in = din("x_loc", [NT, D])
    mem_in = din("mem_b", [MEM, D])
    flags_in = din("flags", [128, 2])
    gains_in = din("gains", [DEPTH, 4, 128, D])
    w_in_e = din("even_w_in", [NEV, D, 3072])
    convw_e = din("even_conv_l", [NEV, 128, 12])
    qkg_e = din("even_qk_gain_l", [NEV, 128, 2])
    lam_e = din("even_lambda_l", [NEV, 128, 256])
    subln_e = din("even_subln_l", [NEV, 128, 128])
    w_out_e = din("even_w_out", [NEV, D, D])
    w_qkv_o = din("odd_w_qkv", [max(NOD, 1), D, 3072])
    w_out_o = din("odd_w_out", [max(NOD, 1), D, D])
    x_wq = din("x_w_q", [DEPTH, D, D])
    x_wkv = din("x_w_kv", [DEPTH, D, 2 * D])
    x_g = din("x_gain_l", [DEPTH, 128, 4])
    x_wo = din("x_w_out", [DEPTH, D, D])
    f_wup = din("ffn_w_up", [DEPTH, D, 2 * DFF])
    f_conv = din("ffn_conv_l", [DEPTH, 128, 44 * 3])
    f_wdn = din("ffn_w_down", [DEPTH, DFF, D])
    ident_in = din("ident", [128, 128])
    bdiag_in = din("bdiag", [128, 128])
    maskneg_in = din("maskneg", [128, 128])
    mask01_in = din("mask01", [128, 128])
    lq_in = din("alibi_lq", [3, 128])
    rk_in = din("alibi_rk", [2, 3, 512])
    ab_in = din("alibi_bias", [128, 2 * 32])
    out = P.dram("out", [TH, D], F32, kind="ExternalOutput")

    X = P.dram("X", [NT, D], F32)
    YA = P.dram("YA", [512, NT], BF16)
    TW = min(512, TH)
    NTI = TH // TW
    QKR = {0: 1024, 1: 2048}
    C1qk = {p_: P.dram(f"C1qk_{p_}", [NTI * QKR[p_], TW], BF16) for p_ in (0, 1)}
    G1qk = {p_: P.dram(f"G1qk_{p_}", [NTI * QKR[p_] * 2, TW], BF16) for p_ in (0, 1)}
    C1v = {0: P.dram("C1v_e", [TH, 512], BF16), 1: P.dram("C1v_o", [TH, 1024], BF16)}
    G1v = {0: P.dram("G1v_e", [2 * TH, 512], BF16), 1: P.dram("G1v_o", [2 * TH, 1024], BF16)}
    C2R = {0: 256, 1: 512}
    C2B = {0: min(8, EB), 1: min(4, EB)}
    C2N = {p_: (EB + C2B[p_] - 1) // C2B[p_] for p_ in (0, 1)}
    C2 = {p_: P.dram(f"C2_{p_}", [C2N[p_] * C2R[p_], C2B[p_] * 128], BF16) for p_ in (0, 1)}
    G2 = {p_: P.dram(f"G2_{p_}", [C2N[p_] * 2 * C2R[p_], C2B[p_] * 128], BF16) for p_ in (0, 1)}
    CRV = {0: min(512, TH), 1: min(256, TH)}

    def c2_ap(p_, r0, eb):
        k, off = eb // C2B[p_], (eb % C2B[p_]) * 128
        return C2[p_][k * C2R[p_] + r0:k * C2R[p_] + r0 + 128, off:off + 128]

    def g2_ap(p_, src, eb):
        k, off = eb // C2B[p_], (eb % C2B[p_]) * 128
        base = k * 2 * C2R[p_] + src * C2R[p_]
        return G2[p_][base:base + C2R[p_], off:off + 128]

    def a2_issue(p_, eb):
        if eb % C2B[p_] == C2B[p_] - 1 or eb == EB - 1:
            k = eb // C2B[p_]
            P.collective("AllGather", rg, C2[p_][k * C2R[p_]:(k + 1) * C2R[p_], :],
                         G2[p_][k * 2 * C2R[p_]:(k + 1) * 2 * C2R[p_], :],
                         r=[C2[p_].key(k * C2B[p_], (k + 1) * C2B[p_])], w=[G2[p_].key(k * C2B[p_], (k + 1) * C2B[p_])])

    def g1qk_row(p_, ti, src, row):
        nrc = QKR[p_] // 512
        return ((ti * nrc + row // 512) * 2 + src) * 512 + row % 512

    def a1_issue(p_, ti):
        nrc = QKR[p_] // 512
        for rc in range(nrc):
            i0 = ti * QKR[p_] + rc * 512
            o0 = (ti * nrc + rc) * 1024
            P.collective("AllGather", rg, C1qk[p_][i0:i0 + 512, :], G1qk[p_][o0:o0 + 1024, :],
                         r=[C1qk[p_].key(ti)], w=[G1qk[p_].key(ti)])
        crv = CRV[p_]
        for k in range(ti * TW // crv, (ti + 1) * TW // crv):
            P.collective("AllGather", rg, C1v[p_][k * crv:(k + 1) * crv, :], G1v[p_][k * 2 * crv:(k + 1) * 2 * crv, :],
                         r=[C1v[p_].key(ti)], w=[G1v[p_].key(ti)])

    def g1v_row(p_, gb):
        src, lbk = gb // NBH, gb % NBH
        k = (lbk * 128) // CRV[p_]
        return (k * 2 + src) * CRV[p_] + (lbk * 128 - k * CRV[p_])

    WB = {}
    for l_ in range(DEPTH):
        WB[l_] = {"in": P.dram(f"wb_in{l_}", [D, 3072], BF16), "mo": P.dram(f"wb_mo{l_}", [D, D], BF16),
                  "q": P.dram(f"wb_q{l_}", [D, D], BF16), "kv": P.dram(f"wb_kv{l_}", [D, 2 * D], BF16),
                  "xo": P.dram(f"wb_xo{l_}", [D, D], BF16), "up": P.dram(f"wb_up{l_}", [11 * 128, 4096], BF16),
                  "dn": P.dram(f"wb_dn{l_}", [DFF, D], BF16)}

    def cast_layer(l_):
        ev = l_ % 2 == 0
        srcs = {"in": (w_in_e[l_ // 2] if ev else w_qkv_o[l_ // 2]), "kv": x_wkv[l_],
                "mo": (w_out_e[l_ // 2] if ev else w_out_o[l_ // 2]), "q": x_wq[l_], "xo": x_wo[l_],
                "dn": f_wdn[l_], "up": f_wup[l_]}
        pieces = []
        for nm in ("in", "kv", "mo", "q", "xo", "dn", "up"):
            src = srcs[nm]
            dstb = WB[l_][nm]
            nrow = DFF if nm == "dn" else D
            for k in range(nrow // 128):
                if nm == "up":
                    dap = dstb.t.rearrange("(j p) (g k n) -> p g j k n", p=128, g=2, k=8)[:, :, :, k, :]
                    sap = src[k * 128:(k + 1) * 128, :].rearrange("p (g j n) -> p g j n", g=2, j=11)
                else:
                    dap = dstb[k * 128:(k + 1) * 128, :]
                    sap = src[k * 128:(k + 1) * 128, :]
                pieces.append((dstb, k, dap, sap))
        return pieces

    cast_q = []

    def issue_casts(n, extra_r=()):
        for _ in range(min(n, len(cast_q))):
            dstb, k, dap, sap = cast_q.pop(0)
            P.dma("pool", dap, sap, r=list(extra_r), w=[dstb.key(k)])

    ARENA_N = 32768
    arena = P.sbuf("arena", [128, ARENA_N], BF16)

    def av(lo_, cnt_, pat=None, **kw):
        ap = arena.t[:, lo_:lo_ + cnt_]
        if pat is not None:
            ap = ap.rearrange(pat, **kw)
        return View(ap, (arena, lo_, lo_ + cnt_))

    xt = P.sbuf("xt", [128, 4, D], F32)
    xt_b = P.sbuf("xt_b", [128, 4, D], F32)
    xcur = {"t": xt, "n": 0}

    def XT():
        return xcur["t"]
    hbf = [P.sbuf(f"hbf{i}", [128, D], BF16) for i in range(2)]
    hT = [P.sbuf(f"hT{i}", [128, 8, 512], BF16) for i in range(2)]
    junk = P.sbuf("junk", [128, D], BF16)
    ss = P.sbuf("ss", [128, 8], F32)
    lnt = P.sbuf("lnt", [128, 8], F32)
    rstd = P.sbuf("rstd", [128, 8], F32)
    Gt = [P.sbuf(f"G{i}", [128, D], F32) for i in range(3)]
    ident = P.sbuf("identb", [128, 128], BF16)
    ones = P.sbuf("onesb", [128, 128], BF16)
    bdiag = P.sbuf("bdiagb", [128, 128], BF16)
    maskneg = P.sbuf("masknegb", [128, 128], BF16)
    mask01 = P.sbuf("mask01f", [128, 128], F32)
    onesf = P.sbuf("onesf", [128, 512], F32)
    zer = P.sbuf("zer", [128, 128], BF16)
    flags = P.sbuf("flagst", [128, 2], F32)
    lq = P.sbuf("lq", [32, 128], BF16)
    rk = P.sbuf("rk", [32, 2, 512], BF16)
    abias = P.sbuf("abias", [128, 64], F32)
    f32t = [P.sbuf(f"f32t{i}", [128, 514], F32) for i in range(8)]
    obt = [P.sbuf(f"obt{i}", [128, 256], BF16) for i in range(2)]
    obt2 = [P.sbuf(f"obt2{i}", [128, 256], BF16) for i in range(2)]
    rsm = P.sbuf("rsm", [128, 16], F32)
    smalls = [P.sbuf(f"smalls{i}", [128, 16], F32) for i in range(2)]
    dsums = [P.sbuf(f"dsums{i}", [128, 2, 16], F32) for i in range(2)]
    bf16t = [P.sbuf(f"bf16t{i}", [128, 1024], BF16) for i in range(4)]
    sbt = [View(xt.t[:, j_, h_ * 512:(h_ + 1) * 512], (xt, j_ * 1024 + h_ * 512, j_ * 1024 + (h_ + 1) * 512))
           for j_ in range(4) for h_ in range(2)]
    sbt += [View(f32t[i].t[:, 0:512], (f32t[i], 0, 512)) for i in range(8)]
    sbt += [P.sbuf(f"sbt{i}", [128, 512], F32) for i in range(2)]
    sbb = bf16t + hbf
    stage = [P.sbuf(f"stage{i}", [128, 1024], BF16) for i in range(2)]
    small = P.sbuf("small", [128, 64], F32)
    cwe = P.sbuf("cwe", [128, 12], F32)
    qkg = P.sbuf("qkg", [128, 2], F32)
    lamt = P.sbuf("lamt", [128, 256], F32)
    subg = P.sbuf("subg", [128, 128], F32)
    xg = P.sbuf("xg", [128, 4], F32)
    fcw = P.sbuf("fcw", [128, 132], F32)
    carry = P.sbuf("carry", [128, 44, 2], F32)
    AT = P.sbuf("AT", [128, 22, 512], BF16)
    XO = av(24576, 4096, "p (k n) -> p k n", k=8)
    knT = av(28672, 2048, "p (k n) -> p k n", k=8)
    vx = av(30720, 2048, "p (k n) -> p k n", k=2)
    dsum = P.sbuf("dsum", [128, 2, 16], F32)
    Rt = [P.sbuf(f"Rt{i}", [128, 1], F32) for i in range(2)]
    nbias = [P.sbuf(f"nbias{i}", [128, 1], F32) for i in range(2)]

    psr = [P.psum(f"psr{i}", [128, 512], F32) for i in range(4)]
    pso = [P.psum(f"pso{i}", [128, 512], F32) for i in range(2)]
    pst = [P.psum(f"pst{i}", [128, 1024], BF16) for i in range(2)]
    cnt = {"ps": 0, "pt": 0, "f": 0, "b": 0, "h": 0, "e": 0}

    def nps():
        cnt["ps"] += 1
        return (psr + pso)[cnt["ps"] % 6]

    def npt():
        cnt["pt"] += 1
        return pst[cnt["pt"] % 2]

    def nf():
        cnt["f"] += 1
        return f32t[cnt["f"] % 8]

    def nbf():
        cnt["b"] += 1
        return bf16t[cnt["b"] % 4]

    def VW3(buf, j):
        return View(buf.t[:, j, :], (buf, j, j + 1))

    def VW(buf, lo, hi):
        return View(buf.t[:, lo:hi], (buf, lo, hi))

    def run_interleaved(gens):
        gens = list(gens)
        while gens:
            for g_ in list(gens):
                try:
                    next(g_)
                except StopIteration:
                    gens.remove(g_)

    def evac_eng():
        cnt["e"] += 1
        return "act" if cnt["e"] % 2 else "dve"

    for dst, src in ((ident, ident_in), (bdiag, bdiag_in), (maskneg, maskneg_in)):
        P.dma("pool", dst[:], src[:], r=[src], w=[dst])
    P.dma("sp", mask01[:], mask01_in[:], r=[mask01_in], w=[mask01])
    P.dma("sp", flags[:], flags_in[:], r=[flags_in], w=[flags])
    P.op("pool", lambda e: e.memset(lq[:], 0.0), w=[lq])
    P.op("pool", lambda e: e.memset(rk[:], 0.0), w=[rk])
    P.dma("pool", lq[0:3, :], lq_in[:], r=[lq_in], w=[lq])
    P.dma("sp", abias[:], ab_in[:], r=[ab_in], w=[abias])
    for hh in range(2):
        P.dma("pool", rk[0:3, hh, :], rk_in[hh], r=[rk_in], w=[rk])
    P.op("pool", lambda e: e.memset(ones[:], 1.0), w=[ones])
    P.op("pool", lambda e: e.memset(onesf[:], 1.0), w=[onesf])
    P.op("pool", lambda e: e.memset(zer[:], 0.0), w=[zer])
    for par in (0, 1):
        for rr in range((256, 512)[par] // 128):
            P.dma("sp", c2_ap(par, rr * 128, 0), zer[:], r=[zer], w=[C2[par].key(0)])
    f0 = flags[:, 0:1]
    f1 = flags[:, 1:2]

    tiles_all = [(0, 1)] + [(1 + 4 * i, min(4, NBH - 4 * i)) for i in range((NBH + 3) // 4)]

    def load_w(dst, srcb, ncols, nkc=8, c0=0):
        P.dma("sp", dst.ap, srcb[0:nkc * 128, c0:c0 + ncols].rearrange("(k p) n -> p k n", p=128), r=[srcb], w=[dst])

    def load_x(src, b0, nb):
        xcur["n"] += 1
        buf = (xt, xt_b)[xcur["n"] % 2]
        P.dma("sp", buf[:, 0:nb, :], src[b0 * 128:(b0 + nb) * 128, :].rearrange("(j p) d -> p j d", p=128),
              r=[src.key(b0, b0 + nb)], w=[buf])
        return buf

    def x_tiles(src, tl_):
        nxt = load_x(src, *tl_[0])
        for i_, (b0_, nb_) in enumerate(tl_):
            xcur["t"] = nxt
            if i_ + 1 < len(tl_):
                nxt = load_x(src, *tl_[i_ + 1])
            yield b0_, nb_

    def norm_T(nb, G):
        cnt["h"] += 1
        h = hT[cnt["h"] % 2]
        for j in range(nb):
            ACT(P, junk[:], XT()[:, j, :], AF.Square, r=[XT()], w=[junk], accum_out=ss[:, j:j + 1])
        P.op("dve", lambda e: e.tensor_copy(lnt[:, 0:nb], ss[:, 0:nb]), r=[junk, ss], w=[lnt])
        ACT(P, lnt[:, 0:nb], lnt[:, 0:nb], AF.Ln, r=[lnt], w=[lnt], scale=1.0 / D, bias=EPS)
        ACT(P, rstd[:, 0:nb], lnt[:, 0:nb], AF.Exp, r=[lnt], w=[rstd], scale=-0.5)
        for j in range(nb):
            hb = hbf[j % 2]
            STT(P, hb[:], XT()[:, j, :], rstd[:, j:j + 1], G[:], ALU.mult, ALU.mult, r=[XT(), rstd, G], w=[hb])
            pt = npt()
            for kc in range(8):
                TR(P, pt[:, kc * 128:(kc + 1) * 128], hb[:, kc * 128:(kc + 1) * 128], ident[:], r=[hb, ident], w=[pt])
            CP(P, evac_eng(), h[:, :, j * 128:(j + 1) * 128], pt[:].rearrange("p (k t) -> p k t", k=8), r=[pt], w=[h])
        return h

    def proj_fm(W, col, h, N, nkc=8):
        ps = nps()
        for kc in range(nkc):
            MM(P, ps[:, 0:N], W.ap[:, kc, col:col + 128], h[:, kc, 0:N], kc == 0, kc == nkc - 1, r=[W, h], w=[ps])
        return ps

    def proj_tm(W, col, h, j, ncol=512, nkc=8):
        ps = nps()
        for kc in range(nkc):
            MM(P, ps[:, 0:ncol], h[:, kc, j * 128:(j + 1) * 128], W.ap[:, kc, col:col + ncol], kc == 0, kc == nkc - 1,
               r=[W, h], w=[ps])
        return ps

    def rstd_from_ss(pss, N, dim):
        t = nf()
        ACT(P, t[:, 0:N], pss[:, 0:N], AF.Ln, r=[pss], w=[t], scale=1.0 / dim, bias=EPS)
        ACT(P, t[:, 0:N], t[:, 0:N], AF.Exp, r=[t], w=[t], scale=-0.5)
        return t

    def conv3(U, N, wap, k0):
        t = nf()
        ACT(P, t[:, 0:N], U[:, 2:2 + N], AF.Copy, r=[U, wap], w=[t], scale=wap[:, k0 + 2:k0 + 3])
        STT(P, t[:, 0:N], U[:, 1:1 + N], wap[:, k0 + 1:k0 + 2], t[:, 0:N], ALU.mult, ALU.add, r=[U, t], w=[t])
        STT(P, t[:, 0:N], U[:, 0:N], wap[:, k0:k0 + 1], t[:, 0:N], ALU.mult, ALU.add, r=[U, t], w=[t])
        return t

    def resid_store(xsrc_tile, j, ps_list, dst, dst_row0, blk_is_halo):
        for nh, ps in enumerate(ps_list):
            TT(P, "dve", XT()[:, j, nh * 512:(nh + 1) * 512], ps[:, 0:512], XT()[:, j, nh * 512:(nh + 1) * 512], ALU.add,
               r=[ps, XT()], w=[XT()])
        if blk_is_halo:
            TS(P, "pool", XT()[:, j, :], XT()[:, j, :], f1, None, ALU.mult, None, r=[XT(), flags], w=[XT()])

    Xsrc = x_in
    last_out = []
    for l in range(DEPTH):
        even = l % 2 == 0
        par = 0 if even else 1
        e = l // 2
        nq = 512 if even else 1024
        nv = 512 if even else 1024
        lam_init = 0.8 - 0.6 * math.exp(-0.3 * l)

        W = av(0, 8 * 3072, "p (k n) -> p k n", k=8)
        if l == 0:
            cast_q.extend(cast_layer(0))
            issue_casts(8)
        load_w(W, WB[l]["in"], 3072)
        if l == 0:
            issue_casts(1000, extra_r=[W])
        P.dma("sp", Gt[0][:], gains_in[l, 0], r=[gains_in], w=[Gt[0]])
        if even:
            P.dma("sp", cwe[:], convw_e[e], r=[convw_e], w=[cwe])
            P.dma("sp", qkg[:], qkg_e[e], r=[qkg_e], w=[qkg])
            P.dma("sp", lamt[:], lam_e[e], r=[lam_e], w=[lamt])
            P.dma("sp", subg[:], subln_e[e], r=[subln_e], w=[subg])
            P.op("pool", lambda e_: e_.memset(carry[:], 0.0), w=[carry])
        for (b0, nb) in x_tiles(Xsrc, tiles_all if even else tiles_all[1:]):
            halo = b0 == 0
            N = nb * 128
            h = norm_T(nb, Gt[0])
            if even:
                for j in range(4):
                    pb = proj_fm(W, j * 128, h, N)
                    pc = proj_fm(W, 512 + j * 128, h, N)
                    px = proj_fm(W, 1024 + j * 128, h, N)
                    csb = nf()
                    CP(P, "act", csb[:, 0:N], pc[:, 0:N], r=[pc], w=[csb])
                    U = nf()
                    TT(P, "dve", U[:, 2:2 + N], px[:, 0:N], csb[:, 0:N], ALU.mult, r=[px, csb], w=[U])
                    CP(P, "dve", U[:, 0:2], carry[:, j, :], r=[carry], w=[U])
                    t = conv3(U, N, cwe, j * 3)
                    CP(P, "dve", carry[:, j, :], U[:, N:N + 2], r=[U], w=[carry])
                    ya = nbf()
                    TT(P, "dve", ya[:, 0:N], pb[:, 0:N], t[:, 0:N], ALU.mult, r=[pb, t], w=[ya])
                    P.dma("sp", YA[j * 128:(j + 1) * 128, b0 * 128:b0 * 128 + N], ya[:, 0:N], r=[ya],
                          w=[YA.key(b0, b0 + nb)])
                if halo:
                    continue
                for c in range(8):
                    ps = proj_fm(W, 1536 + c * 128, h, N)
                    sq = nbf()
                    ACT(P, sq[:, 0:N], ps[:, 0:N], AF.Square, r=[ps], w=[sq])
                    pss = nps()
                    MM(P, pss[:, 0:N], bdiag[:], sq[:, 0:N], True, True, r=[bdiag, sq], w=[pss])
                    rs = rstd_from_ss(pss, N, 64)
                    qn = nbf()
                    STT(P, qn[:, 0:N], ps[:, 0:N], qkg[:, (c // 4):(c // 4) + 1], rs[:, 0:N], ALU.mult, ALU.mult,
                        r=[ps, qkg, rs], w=[qn])
                    ti_ = (b0 - 1) // 4
                    P.dma("sp", C1qk[0][ti_ * 1024 + c * 128:ti_ * 1024 + (c + 1) * 128, 0:N], qn[:, 0:N], r=[qn],
                          w=[C1qk[0].key(ti_)])
                for j in range(nb):
                    ps = proj_tm(W, 2560, h, j)
                    vt = nbf()
                    CP(P, evac_eng(), vt[:, 0:512], ps[:, 0:512], r=[ps], w=[vt])
                    P.dma("sp", C1v[0][(b0 - 1 + j) * 128:(b0 + j) * 128, :], vt[:, 0:512], r=[vt], w=[C1v[0].key((b0 - 1) // 4)])
                a1_issue(0, (b0 - 1) // 4)
            else:
                for c in range(16):
                    ps = proj_fm(W, c * 128, h, N)
                    qn = nbf()
                    CP(P, evac_eng(), qn[:, 0:N], ps[:, 0:N], r=[ps], w=[qn])
                    ti_ = (b0 - 1) // 4
                    P.dma("sp", C1qk[1][ti_ * 2048 + c * 128:ti_ * 2048 + (c + 1) * 128, 0:N], qn[:, 0:N], r=[qn],
                          w=[C1qk[1].key(ti_)])
                for j in range(nb):
                    vt = nbf()
                    for nh in range(2):
                        ps = proj_tm(W, 2048 + nh * 512, h, j)
                        CP(P, evac_eng(), vt[:, nh * 512:(nh + 1) * 512], ps[:, 0:512], r=[ps], w=[vt])
                    P.dma("sp", C1v[1][(b0 - 1 + j) * 128:(b0 + j) * 128, :], vt[:, 0:1024], r=[vt], w=[C1v[1].key((b0 - 1) // 4)])
                a1_issue(1, (b0 - 1) // 4)


        if even:
            t = nf()
            TT(P, "dve", t[:, 0:64], lamt[:, 0:64], lamt[:, 64:128], ALU.mult, r=[lamt], w=[t])
            P.op("dve", lambda e_, t=t: e_.tensor_reduce(small[:, 0:1], t[:, 0:64], AX.X, ALU.add), r=[t], w=[small])
            t2 = nf()
            TT(P, "dve", t2[:, 0:64], lamt[:, 128:192], lamt[:, 192:256], ALU.mult, r=[lamt], w=[t2])
            P.op("dve", lambda e_, t2=t2: e_.tensor_reduce(small[:, 1:2], t2[:, 0:64], AX.X, ALU.add), r=[t2, small], w=[small])
            ACT(P, small[:, 2:4], small[:, 0:2], AF.Exp, r=[small], w=[small])
            TT(P, "dve", small[:, 4:5], small[:, 3:4], small[:, 2:3], ALU.subtract, r=[small], w=[small])
            TS(P, "dve", small[:, 5:6], small[:, 4:5], -lam_init, None, ALU.add, None, r=[small], w=[small])
        P.dma("sp", Gt[2][:], gains_in[l, 2], r=[gains_in], w=[Gt[2]])
        P.dma("sp", xg[:], x_g[l], r=[x_g], w=[xg])
        P.dma("sp", XT()[:, 0:2, :], mem_in[:].rearrange("(j p) d -> p j d", p=128), r=[mem_in], w=[XT()])
        hm = norm_T(2, Gt[2])
        for qd in range(4):
            Wk4 = av(24576, 4096, "p (k n) -> p k n", k=8)
            load_w(Wk4, WB[l]["kv"], 512, c0=qd * 512)
            if qd < 2:
                for hd2 in range(2):
                    hd = 2 * qd + hd2
                    pk = [proj_fm(Wk4, (2 * hd2 + k2) * 128, hm, MEM) for k2 in range(2)]
                    pss = nps()
                    for k2 in range(2):
                        sq = nbf()
                        ACT(P, sq[:, 0:MEM], pk[k2][:, 0:MEM], AF.Square, r=[pk[k2]], w=[sq])
                        MM(P, pss[:, 0:MEM], ones[:], sq[:, 0:MEM], k2 == 0, k2 == 1, r=[ones, sq], w=[pss])
                    rs = rstd_from_ss(pss, MEM, 256)
                    for k2 in range(2):
                        STT(P, knT[:, 2 * hd + k2, :], pk[k2][:, 0:MEM], xg[:, 2 + k2:3 + k2], rs[:, 0:MEM], ALU.mult, ALU.mult,
                            r=[pk[k2], xg, rs], w=[knT])
            else:
                for mc in range(2):
                    ps = proj_tm(Wk4, 0, hm, mc)
                    CP(P, evac_eng(), vx[:, mc, (qd - 2) * 512:(qd - 1) * 512], ps[:, 0:512], r=[ps], w=[vx])
        ngroups = 1 if even else 2
        QT = av(0, 8192, "p (c t) -> p c t", c=2)
        KT = av(8192, 8192, "p (c t) -> p c t", c=2)
        VV = av(16384, NB * 256, "p (b n) -> p b n", n=256)
        for grp in range(ngroups):
            npiece = NTI
            for which, dstv, rowbase in ((0, QT, 0), (1, KT, nq)):
                for src in range(2):
                    for ch in range(2):
                        for pc in range(npiece):
                            c0 = pc * TW
                            cn = TW
                            for rr in range(2):
                                row = g1qk_row(par, pc, src, rowbase + rr * (nq // 2) + grp * 256 + ch * 128)
                                P.dma("sp", stage[rr][:, 0:cn], G1qk[par][row:row + 128, 0:cn], r=[G1qk[par]],
                                      w=[stage[rr]])
                            TS(P, "dve", stage[0][:, 0:cn], stage[0][:, 0:cn], f0, None, ALU.mult, None,
                               r=[stage[0], flags], w=[stage[0]])
                            STT(P, dstv.ap[:, ch, src * TH + c0:src * TH + c0 + cn], stage[1][:, 0:cn], f1, stage[0][:, 0:cn],
                                ALU.mult, ALU.add, r=[stage[0], stage[1], flags], w=[dstv])
            vstep = min(4, CRV[par] // 128)
            for b8 in range(0, NB, vstep):
                bn = vstep
                vr0 = g1v_row(par, b8)
                g1v = G1v[par][vr0:vr0 + bn * 128, :].rearrange("(b p) n -> p b n", p=128)
                for rr in range(2):
                    col = rr * (nv // 2) + grp * 256
                    P.dma("sp", stage[rr][:, 0:bn * 256].rearrange("p (b n) -> p b n", n=256),
                          g1v[:, :, col:col + 256], r=[G1v[par]], w=[stage[rr]])
                TS(P, "dve", stage[0][:, 0:bn * 256], stage[0][:, 0:bn * 256], f0, None, ALU.mult, None,
                   r=[stage[0], flags], w=[stage[0]])
                STT(P, VV.ap[:, b8:b8 + bn, :].rearrange("p b n -> p (b n)") if False else VV.ap[:, b8:b8 + bn, :],
                    stage[1][:, 0:bn * 256].rearrange("p (b n) -> p b n", n=256), f1,
                    stage[0][:, 0:bn * 256].rearrange("p (b n) -> p b n", n=256),
                    ALU.mult, ALU.add, r=[stage[0], stage[1], flags], w=[VV])

            sb_items = []
            for g in range(NB):
                gc = slice(g * 128, (g + 1) * 128)
                nch = g // 4 + 1
                if even:
                    def da_stream(hh, g=g, gc=gc, nch=nch):
                        po = [pso[hh], psr[2 + hh]]
                        E, ET = bf16t[hh], bf16t[2 + hh]
                        sm = smalls[hh]
                        ds_ = dsums[hh]
                        jk = VW(junk, hh * 128, hh * 128 + 128)
                        for c in range(nch):
                            nk = 512 if c < nch - 1 else (g % 4 + 1) * 128
                            kc_ = slice(c * 512, c * 512 + nk)
                            diag = c == nch - 1
                            u_ = g - 4 * c
                            for m in range(2):
                                pb_ = slice(m * 64, (m + 1) * 64)
                                ps = psr[hh]
                                MM(P, ps[:, 0:nk], QT.ap[pb_, hh, gc], KT.ap[pb_, hh, kc_], True, False, r=[QT, KT], w=[ps])
                                MM(P, ps[:, 0:nk], lq[0:32, :], rk[0:32, hh, 0:nk], False, not diag, r=[lq, rk], w=[ps])
                                if diag:
                                    MM(P, ps[:, nk - 128:nk], ident[:], maskneg[:], False, True, r=[ident, maskneg], w=[ps])
                                yield
                                ACT(P, E[:, 0:nk], ps[:, 0:nk], AF.Exp, r=[ps, abias], w=[E, ds_], scale=0.125,
                                    bias=abias[:, hh * 32 + u_:hh * 32 + u_ + 1], accum_out=ds_[:, m, c:c + 1])
                                yield
                                pt = pst[hh]
                                for kb in range(nk // 128):
                                    TR(P, pt[:, kb * 128:(kb + 1) * 128], E[:, kb * 128:(kb + 1) * 128], ident[:], r=[E, ident],
                                       w=[pt])
                                yield
                                CP(P, ("act" if m == 0 else "dve"), ET[:, 0:nk], pt[:, 0:nk], r=[pt], w=[ET])
                                yield
                                for kb in range(nk // 128):
                                    lastmm = diag and kb == nk // 128 - 1
                                    MM(P, po[m][:, 0:128], ET[:, kb * 128:(kb + 1) * 128],
                                       VV.ap[:, c * 4 + kb, hh * 128:(hh + 1) * 128], c == 0 and kb == 0, lastmm,
                                       r=[ET, VV], w=[po[m]])
                                yield
                        P.op("dve", lambda e_, nch=nch: e_.tensor_reduce(sm[:, 8:10], ds_[:, :, 0:nch], AX.X, ALU.add),
                             r=[ds_], w=[sm])
                        P.op("dve", lambda e_: e_.reciprocal(sm[:, 10:12], sm[:, 8:10]), r=[sm], w=[sm])
                        TT(P, "dve", sm[:, 12:13], sm[:, 11:12], small[:, 5:6], ALU.mult, r=[sm, small], w=[sm])
                        yield
                        o = f32t[6 + hh]
                        TS(P, "dve", o[:, 0:128], po[0][:, 0:128], sm[:, 10:11], None, ALU.mult, None, r=[po[0], sm], w=[o])
                        STT(P, o[:, 0:128], po[1][:, 0:128], sm[:, 12:13], o[:, 0:128], ALU.mult, ALU.add,
                            r=[po[1], sm, o], w=[o])
                        yield
                        ACT(P, jk.ap, o[:, 0:128], AF.Square, r=[o], w=[jk, sm], accum_out=sm[:, 13:14])
                        ACT(P, sm[:, 14:15], sm[:, 13:14], AF.Ln, r=[sm], w=[sm], scale=1.0 / 128, bias=EPS)
                        ACT(P, sm[:, 15:16], sm[:, 14:15], AF.Exp, r=[sm], w=[sm], scale=-0.5)
                        yield
                        TS(P, "dve", o[:, 0:128], o[:, 0:128], sm[:, 15:16], 1.0 - lam_init, ALU.mult, ALU.mult,
                           r=[o, sm], w=[o])
                        TT(P, "dve", E[:, 0:128], o[:, 0:128], subg[:], ALU.mult, r=[o, subg], w=[E])
                        yield
                        pt = pst[hh]
                        TR(P, pt[:, 0:128], E[:, 0:128], ident[:], r=[E, ident], w=[pt])
                        yield
                        CP(P, "act", ET[:, 0:128], pt[:, 0:128], r=[pt], w=[ET])
                        P.dma("sp", c2_ap(0, hh * 128, g + 1), ET[:, 0:128], r=[ET],
                              w=[C2[0].key(g + 1)])
                        yield

                    run_interleaved([da_stream(0), da_stream(1)])
                    a2_issue(par, g + 1)
                else:
                    ob = obt[g % 2]

                    def sb_stream(si, hh, g=g, gc=gc, nch=nch, ob=ob):
                        ch = hh // 2
                        pb_ = slice((hh % 2) * 64, (hh % 2) * 64 + 64)
                        po = (psr[3], pso[0], pso[1])[si]
                        tl = [(sbt[si * 6 + p3 * 3], sbt[si * 6 + p3 * 3 + 1], sbt[si * 6 + p3 * 3 + 2]) for p3 in range(2)]
                        av_ = [VW(sbb[2 * si], p3 * 512, p3 * 512 + 512) for p3 in range(2)]
                        aTv = [VW(sbb[2 * si + 1], p3 * 512, p3 * 512 + 512) for p3 in range(2)]
                        nbv = [VW(rsm, 4 * si + i, 4 * si + i + 1) for i in range(2)]
                        nci = nch

                        def geom(c):
                            diag = c == nch - 1
                            nk = 512 if not diag else (g % 4 + 1) * 128
                            return diag, nk

                        def front(ci, c):
                            diag, nk = geom(c)
                            kc_ = slice(c * 512, c * 512 + nk)
                            ex, sp_, Fc = tl[ci % 2]
                            pz = psr[si]
                            MM(P, pz[:, 0:nk], QT.ap[pb_, ch, gc], KT.ap[pb_, ch, kc_], True, True, r=[QT, KT], w=[pz])
                            yield
                            ACT(P, ex[:, 0:nk], pz[:, 0:nk], AF.Exp, r=[pz], w=[ex], scale=-0.125)
                            yield
                            ACT(P, sp_[:, 0:nk], ex[:, 0:nk], AF.Ln, r=[ex], w=[sp_], bias=1.0)
                            yield
                            STT(P, ex[:, 0:nk], pz[:, 0:nk], 0.125, sp_[:, 0:nk], ALU.mult, ALU.add, r=[pz, sp_], w=[ex])
                            if diag:
                                TT(P, "pool", ex[:, nk - 128:nk], ex[:, nk - 128:nk], mask01[:], ALU.mult, r=[ex, mask01], w=[ex])
                            yield
                            P.op("dve", lambda e_, Fc=Fc, ex=ex, nk=nk: e_.tensor_tensor_scan(
                                Fc[:, 0:nk], onesf[:, 0:nk], ex[:, 0:nk], 0.0, ALU.mult, ALU.add), r=[onesf, ex], w=[Fc])
                            yield

                        def back(ci, c):
                            diag, nk = geom(c)
                            ex, sp_, Fc = tl[ci % 2]
                            a = av_[ci % 2]
                            aT = aTv[ci % 2]
                            nb_ = nbv[(ci + 1) % 2]
                            TS(P, "dve", nb_.ap, Fc[:, nk - 1:nk], -1.0, nbv[ci % 2].ap, ALU.mult, ALU.add,
                               r=[Fc, nbv[ci % 2]], w=[nb_])
                            TT(P, "pool", sp_[:, 0:nk], Fc[:, 0:nk], sp_[:, 0:nk], ALU.subtract, r=[Fc, sp_], w=[sp_])
                            yield
                            ACT(P, a.ap[:, 0:nk], sp_[:, 0:nk], AF.Exp, r=[sp_, nb_], w=[a], bias=nb_.ap)
                            if diag:
                                TT(P, "pool", a.ap[:, nk - 128:nk], a.ap[:, nk - 128:nk], mask01[:], ALU.mult, r=[a, mask01], w=[a])
                            yield
                            ptb = npt()
                            for kb in range(nk // 128):
                                TR(P, ptb[:, kb * 128:(kb + 1) * 128], a.ap[:, kb * 128:(kb + 1) * 128], ident[:],
                                   r=[a, ident], w=[ptb])
                            CP(P, ("act" if ci % 2 == 0 else "dve"), aT.ap[:, 0:nk], ptb[:, 0:nk], r=[ptb], w=[aT])
                            yield
                            for kb in range(nk // 128):
                                MM(P, po[:, 0:64], aT.ap[:, kb * 128:(kb + 1) * 128], VV.ap[:, c * 4 + kb, hh * 64:(hh + 1) * 64],
                                   ci == 0 and kb == 0, ci == nci - 1 and kb == nk // 128 - 1, r=[aT, VV], w=[po])
                            yield

                        chunks = list(enumerate(range(nch - 1, -1, -1)))
                        P.op("pool", lambda e_: e_.memset(nbv[0].ap, 0.0), w=[nbv[0]])
                        yield
                        yield from front(*chunks[0])
                        for i in range(len(chunks)):
                            gens = [back(*chunks[i])]
                            if i + 1 < len(chunks):
                                gens.append(front(*chunks[i + 1]))
                            while gens:
                                for gq in list(gens):
                                    try:
                                        next(gq)
                                    except StopIteration:
                                        gens.remove(gq)
                                        continue
                                    yield
                        CP(P, "act", ob[:, hh * 64:(hh + 1) * 64], po[:, 0:64], r=[po], w=[ob])
                        yield

                    def sb_final(g=g, ob=ob, grp=grp):
                      if True:
                        pt = npt()
                        for k2 in range(2):
                            TR(P, pt[:, k2 * 128:(k2 + 1) * 128], ob[:, k2 * 128:(k2 + 1) * 128], ident[:], r=[ob, ident], w=[pt])
                        oT = obt2[g % 2]
                        CP(P, "dve", oT[:, 0:256], pt[:, 0:256], r=[pt], w=[oT])
                        for k2 in range(2):
                            r0 = grp * 256 + k2 * 128
                            P.dma("sp", c2_ap(1, r0, g + 1), oT[:, k2 * 128:(k2 + 1) * 128], r=[oT],
                                  w=[C2[1].key(g + 1)])
                        if grp == ngroups - 1:
                            a2_issue(par, g + 1)


                    sb_items.append((g, sb_stream, sb_final))
            if not even:
                sb_done = {}

                def sb_worker(si):
                    while sb_items_q:
                        g_, hh_, strm, fin = sb_items_q.pop(0)
                        yield from strm(si, hh_)
                        sb_done[g_] = sb_done.get(g_, 0) + 1
                        if sb_done[g_] == 4:
                            fin()
                            yield

                sb_items_q = [(g_, hh_, strm, fin) for (g_, strm, fin) in sb_items for hh_ in range(4)]
                run_interleaved([sb_worker(0), sb_worker(1), sb_worker(2)])
                sb_items.clear()


        Wmo = av(0, 8192, "p (k n) -> p k n", k=8)
        Wq = av(8192, 8192, "p (k n) -> p k n", k=8)
        Wxo = av(16384, 8192, "p (k n) -> p k n", k=8)
        P.dma("sp", Gt[1][:], gains_in[l, 1], r=[gains_in], w=[Gt[1]])
        load_w(Wmo, WB[l]["mo"], D)
        load_w(Wq, WB[l]["q"], D)
        load_w(Wxo, WB[l]["xo"], D)
        myf = 256 if even else 512
        if l + 1 < DEPTH:
            cast_q.extend(cast_layer(l + 1))
        n_per = (len(cast_q) + 2 * len(tiles_all) - 1) // (2 * len(tiles_all))
        for (b0, nb) in x_tiles(Xsrc, tiles_all):
            N = nb * 128
            issue_casts(n_per)
            for j in range(nb):
                lb = b0 + j
                OT = nbf()
                chunks = []
                if even:
                    P.dma("sp", OT[:, 0:512].rearrange("p (c t) -> p c t", c=4),
                          YA[:, lb * 128:(lb + 1) * 128].rearrange("(c p) t -> p c t", p=128), r=[YA.key(lb)], w=[OT])
                    base = 4
                else:
                    base = 0
                nch_src = myf // 128
                for src in range(2):
                    for rr in range(2):
                        eb = lb + rr * NBH
                        P.dma("sp", stage[rr][:, 0:nch_src * 128].rearrange("p (c t) -> p c t", c=nch_src),
                              g2_ap(par, src, eb).rearrange("(c p) t -> p c t", p=128),
                              r=[G2[par]], w=[stage[rr]])
                    TS(P, "dve", stage[0][:, 0:nch_src * 128], stage[0][:, 0:nch_src * 128], f0, None, ALU.mult, None,
                       r=[stage[0], flags], w=[stage[0]])
                    o0 = (base + src * nch_src) * 128
                    STT(P, OT[:, o0:o0 + nch_src * 128], stage[1][:, 0:nch_src * 128], f1, stage[0][:, 0:nch_src * 128],
                        ALU.mult, ALU.add, r=[stage[0], stage[1], flags], w=[OT])
                pss = []
                for nh in range(2):
                    ps = nps()
                    for c in range(8):
                        MM(P, ps[:, 0:512], OT[:, c * 128:(c + 1) * 128], Wmo.ap[:, c, nh * 512:(nh + 1) * 512], c == 0, c == 7,
                           r=[OT, Wmo], w=[ps])
                    pss.append(ps)
                resid_store(None, j, pss, None, None, lb == 0)
            h = norm_T(nb, Gt[1])
            for hd in range(4):
                pq = [proj_fm(Wq, (2 * hd + k2) * 128, h, N) for k2 in range(2)]
                pss = nps()
                for k2 in range(2):
                    sq = nbf()
                    ACT(P, sq[:, 0:N], pq[k2][:, 0:N], AF.Square, r=[pq[k2]], w=[sq])
                    MM(P, pss[:, 0:N], ones[:], sq[:, 0:N], k2 == 0, k2 == 1, r=[ones, sq], w=[pss])
                rs = rstd_from_ss(pss, N, 256)
                qn = nbf()
                for k2 in range(2):
                    STT(P, qn[:, k2 * 512:k2 * 512 + N], pq[k2][:, 0:N], xg[:, k2:k2 + 1], rs[:, 0:N], ALU.mult, ALU.mult,
                        r=[pq[k2], xg, rs], w=[qn])
                E = nbf()
                for mc in range(2):
                    ps = nps()
                    for k2 in range(2):
                        MM(P, ps[:, 0:N], knT[:, 2 * hd + k2, mc * 128:(mc + 1) * 128], qn[:, k2 * 512:k2 * 512 + N], k2 == 0,
                           k2 == 1, r=[knT, qn], w=[ps])
                    ACT(P, E[:, mc * 512:mc * 512 + N], ps[:, 0:N], AF.Exp, r=[ps], w=[E], scale=1.0 / 16)
                pd = nps()
                for mc in range(2):
                    MM(P, pd[:, 0:N], ones[:], E[:, mc * 512:mc * 512 + N], mc == 0, mc == 1, r=[ones, E], w=[pd])
                dr = nf()
                P.op("dve", lambda e_, dr=dr, pd=pd, N=N: e_.reciprocal(dr[:, 0:N], pd[:, 0:N]), r=[pd], w=[dr])
                for dc in range(2):
                    ps = nps()
                    for mc in range(2):
                        MM(P, ps[:, 0:N], vx[:, mc, hd * 256 + dc * 128:hd * 256 + (dc + 1) * 128], E[:, mc * 512:mc * 512 + N],
                           mc == 0, mc == 1, r=[vx, E], w=[ps])
                    TT(P, "dve", XO[:, 2 * hd + dc, 0:N], ps[:, 0:N], dr[:, 0:N], ALU.mult, r=[ps, dr], w=[XO])
            for j in range(nb):
                lb = b0 + j
                pss = []
                for nh in range(2):
                    ps = nps()
                    for c in range(8):
                        MM(P, ps[:, 0:512], XO[:, c, j * 128:(j + 1) * 128], Wxo.ap[:, c, nh * 512:(nh + 1) * 512], c == 0, c == 7,
                           r=[XO, Wxo], w=[ps])
                    pss.append(ps)
                resid_store(None, j, pss, None, None, lb == 0)
            P.dma("sp", X[b0 * 128:(b0 + nb) * 128, :].rearrange("(j p) d -> p j d", p=128), XT()[:, 0:nb, :], r=[XT()],
                  w=[X.key(b0, b0 + nb)])
        Xsrc = X

        Wd = av(0, 22 * 1024, "p (k n) -> p k n", k=22)
        load_w(Wd, WB[l]["dn"], D, nkc=22)
        P.dma("sp", Gt[0][:], gains_in[l, 3], r=[gains_in], w=[Gt[0]])
        P.dma("sp", fcw[:], f_conv[l], r=[f_conv], w=[fcw])
        P.op("pool", lambda e_: e_.memset(carry[:], 0.0), w=[carry])
        wu_i = 0
        lastl = l == DEPTH - 1
        for (b0, nb) in x_tiles(X, tiles_all):
            N = nb * 128
            issue_casts(n_per)
            h = norm_T(nb, Gt[0])
            for jp in range(11):
                wu_i += 1
                Wu = av(22 * 1024 + (wu_i % 2) * 4096, 4096, "p (g k n) -> p g k n", g=2, k=8)
                wlo = 22 * 1024 + (wu_i % 2) * 4096
                P.dma("sp", arena.t[:, wlo:wlo + 4096], WB[l]["up"][jp * 128:(jp + 1) * 128, :],
                      r=[WB[l]["up"]], w=[Wu])
                ch4 = [(2 * jp + q_, gv) for q_ in range(2) for gv in range(2)]
                pss4 = []
                for (jf, gv) in ch4:
                    jo = (jf % 2) * 128
                    ps = nps()
                    for kc in range(8):
                        MM(P, ps[:, 0:N], Wu.ap[:, gv, kc, jo:jo + 128], h[:, kc, 0:N], kc == 0, kc == 7, r=[Wu, h], w=[ps])
                    pss4.append(ps)
                us4 = [nf() for _ in ch4]
                k04 = [(gv * 22 + jf) * 3 for (jf, gv) in ch4]
                cr4 = [VW3(carry, gv * 22 + jf) for (jf, gv) in ch4]
                for i4 in range(4):
                    ACT(P, us4[i4][:, 0:N], pss4[i4][:, 0:N], AF.Copy, r=[pss4[i4], fcw], w=[us4[i4]],
                        scale=fcw[:, k04[i4] + 2:k04[i4] + 3])
                for i4 in range(4):
                    STT(P, us4[i4][:, 1:N], pss4[i4][:, 0:N - 1], fcw[:, k04[i4] + 1:k04[i4] + 2], us4[i4][:, 1:N], ALU.mult, ALU.add,
                        r=[pss4[i4], us4[i4]], w=[us4[i4]])
                for i4 in range(4):
                    STT(P, us4[i4][:, 2:N], pss4[i4][:, 0:N - 2], fcw[:, k04[i4]:k04[i4] + 1], us4[i4][:, 2:N], ALU.mult, ALU.add,
                        r=[pss4[i4], us4[i4]], w=[us4[i4]])
                for i4 in range(4):
                    STT(P, us4[i4][:, 0:1], cr4[i4].ap[:, 1:2], fcw[:, k04[i4] + 1:k04[i4] + 2], us4[i4][:, 0:1], ALU.mult, ALU.add,
                        r=[cr4[i4], us4[i4]], w=[us4[i4]])
                for i4 in range(4):
                    STT(P, us4[i4][:, 0:2], cr4[i4].ap[:, 0:2], fcw[:, k04[i4]:k04[i4] + 1], us4[i4][:, 0:2], ALU.mult, ALU.add,
                        r=[cr4[i4], us4[i4]], w=[us4[i4]])
                for i4 in range(4):
                    CP(P, "dve", cr4[i4].ap, pss4[i4][:, N - 2:N], r=[pss4[i4]], w=[cr4[i4]])
                for q_ in range(2):
                    jf = 2 * jp + q_
                    sg = nf()
                    ACT(P, sg[:, 0:N], us4[2 * q_][:, 0:N], AF.Silu, r=[us4[2 * q_]], w=[sg])
                    TT(P, "pool", AT[:, jf, 0:N], sg[:, 0:N], us4[2 * q_ + 1][:, 0:N], ALU.mult, r=[sg, us4[2 * q_ + 1]], w=[AT])
            for j in range(nb):
                lb = b0 + j
                pss = []
                for nh in range(2):
                    ps = nps()
                    for jf in range(22):
                        MM(P, ps[:, 0:512], AT[:, jf, j * 128:(j + 1) * 128], Wd.ap[:, jf, nh * 512:(nh + 1) * 512], jf == 0,
                           jf == 21, r=[AT, Wd], w=[ps])
                    pss.append(ps)
                resid_store(None, j, pss, None, None, False)
            if not lastl:
                P.dma("sp", X[b0 * 128:(b0 + nb) * 128, :].rearrange("(j p) d -> p j d", p=128), XT()[:, 0:nb, :], r=[XT()],
                      w=[X.key(b0, b0 + nb)])
            elif b0 >= 1:
                last_out.append(P.dma("sp", out[(b0 - 1) * 128:(b0 - 1 + nb) * 128, :].rearrange("(j p) d -> p j d", p=128),
                                      XT()[:, 0:nb, :], r=[XT()], w=[out.key(b0, b0 + nb)]))
    assert not cast_q
    P.emit(final_wait=last_out)
    return nc


_CACHE = {}


def _consts(S):
    NB = S // 128
    ident = np.eye(128, dtype=np.float32)
    bdiag = np.zeros((128, 128), np.float32)
    bdiag[:64, :64] = 1.0
    bdiag[64:, 64:] = 1.0
    t = np.arange(128)[:, None]
    s = np.arange(128)[None, :]
    maskneg = np.where(s <= t, 0.0, -30000.0).astype(np.float32)
    mask01 = (s < t).astype(np.float32)
    pos = np.arange(512)
    lqt = np.stack([np.arange(128), np.ones(128), np.ones(128)]).astype(np.float32)
    slopes = alibi_slopes(4)
    rks, abs_ = [], []
    for h in range(4):
        sl = slopes[h]
        rks.append(np.stack([np.full(pos.shape, -8.0 * sl), 8.0 * sl * (pos % 128), 1024.0 * sl * (pos // 128)])
                   .astype(np.float32))
        abs_.append((-128.0 * sl * np.arange(32)).astype(np.float32))
    return ident, bdiag, maskneg, mask01, lqt, np.stack(rks), np.stack(abs_)


def kernel(**inputs):
    S = CFG["SEQ"]
    DEPTH = CFG["DEPTH"]
    key = (S, DEPTH)
    if key not in _CACHE:
        _CACHE[key] = build(S, DEPTH)
    nc = _CACHE[key]
    f = lambda k: np.ascontiguousarray(np.asarray(inputs[k], dtype=np.float32))
    x = f("x")
    mem = f("mem")
    B = x.shape[0]
    HALF = S // 2
    NEV = (DEPTH + 1) // 2
    ident, bdiag, maskneg, mask01, lqt, rks, abs_ = _consts(S)
    bc = lambda v, n=128: np.ascontiguousarray(np.broadcast_to(v[..., None, :], v.shape[:-1] + (n, v.shape[-1])))
    gains = np.stack([bc(f("norm_mix")), bc(f("norm_xattn")), bc(f("norm_mem")), bc(f("norm_ffn"))], axis=1)
    ec = f("even_conv")
    even_conv_l = np.ascontiguousarray(ec.reshape(NEV, 3, 4, 128).transpose(0, 3, 2, 1).reshape(NEV, 128, 12))
    qg = f("even_q_gain")
    kg = f("even_k_gain")
    even_qk = np.ascontiguousarray(np.stack([np.concatenate([qg, qg], -1), np.concatenate([kg, kg], -1)], -1))
    lam_l = bc(f("even_lambda").reshape(NEV, 256))
    subln_l = bc(f("even_subln"))
    xq = f("x_q_gain")
    xk = f("x_k_gain")
    x_gain_l = np.ascontiguousarray(np.concatenate([xq.reshape(DEPTH, 2, 128).transpose(0, 2, 1),
                                                    xk.reshape(DEPTH, 2, 128).transpose(0, 2, 1)], -1))
    fc = f("ffn_conv")
    ffn_conv_l = np.ascontiguousarray(fc.reshape(DEPTH, 3, 44, 128).transpose(0, 3, 2, 1).reshape(DEPTH, 128, 132))
    shared = {
        "gains": gains, "even_w_in": f("even_w_in"), "even_conv_l": even_conv_l, "even_qk_gain_l": even_qk,
        "even_lambda_l": lam_l, "even_subln_l": subln_l, "even_w_out": f("even_w_out"),
        "odd_w_qkv": f("odd_w_qkv"), "odd_w_out": f("odd_w_out"), "x_w_q": f("x_w_q"), "x_w_kv": f("x_w_kv"),
        "x_gain_l": x_gain_l, "x_w_out": f("x_w_out"), "ffn_w_up": f("ffn_w_up"), "ffn_conv_l": ffn_conv_l,
        "ffn_w_down": f("ffn_w_down"), "ident": ident, "bdiag": bdiag, "maskneg": maskneg, "mask01": mask01,
        "alibi_lq": lqt,
    }
    in_maps = []
    for c in range(8):
        b, r = c // 2, c % 2
        if r == 0:
            xl = np.concatenate([np.zeros((128, D), np.float32), x[b, 0:HALF]], 0)
        else:
            xl = x[b, HALF - 128:S]
        fl = np.zeros((128, 2), np.float32)
        fl[:, r] = 1.0
        m = dict(shared)
        m.update({"x_loc": np.ascontiguousarray(xl), "mem_b": mem[b], "flags": fl,
                  "alibi_rk": np.ascontiguousarray(rks[2 * r:2 * r + 2]),
                  "alibi_bias": np.ascontiguousarray(np.broadcast_to(abs_[2 * r:2 * r + 2].reshape(1, 64), (128, 64)))})
        in_maps.append(m)
    res = run_bass_kernel_spmd(nc, in_maps, core_ids=list(range(8)))
    outp = np.empty((B, S, D), np.float32)
    for c in range(8):
        b, r = c // 2, c % 2
        outp[b, r * HALF:(r + 1) * HALF] = res.results[c]["out"]
    return outp
```
